# Optimizing a Trainium2 kernel written in Bass

```python
import math
import jax, jax.numpy as jnp
from jax import lax
import numpy as np

D_MODEL = 1024
BATCH = 4
SEQ = 8192
DEPTH = 4

GRID_W = 64
CTX_LEN = 256
HEAD_DIM = 64
MIX_WIDTH = D_MODEL
N_GROUPS = 4
GROUP_WIDTH = MIX_WIDTH // N_GROUPS
DIFF_HEADS = GROUP_WIDTH // HEAD_DIM
DIFF_QK_DIM = HEAD_DIM // 2
SWA_HEADS = GROUP_WIDTH // HEAD_DIM
SWA_KV_HEADS = 2
SWA_GROUP = SWA_HEADS // SWA_KV_HEADS
SWA_WINDOW = 128
SWA_BLOCK = 128
NA_HEADS = GROUP_WIDTH // HEAD_DIM
NA_ROWS_MAX = 8
NA_COLS = 16
SCONV_CH = GROUP_WIDTH
SCONV_WIDTH = 3
D_FF = 2816
FFN_CONV_WIDTH = 3
Q_BLOCK = 128
ROPE_THETA = 10000.0
EPS = 1e-6
NEG_INF = -1e30

A_Q = DIFF_HEADS * 2 * DIFF_QK_DIM
A_K = A_Q
A_V = DIFF_HEADS * HEAD_DIM
A_COLS = A_Q + A_K + A_V
B_Q = SWA_HEADS * HEAD_DIM
B_KV = SWA_KV_HEADS * HEAD_DIM
B_COLS = B_Q + 2 * B_KV
C_W = NA_HEADS * HEAD_DIM
C_COLS = 3 * C_W
D_COLS = 3 * SCONV_CH
IN_COLS = A_COLS + B_COLS + C_COLS + D_COLS

kernel_name = "hybrid_parallel_heads_diffusion_trunk"


def rmsnorm(x, w):
    xf = x.astype(jnp.float32)
    y = xf * lax.rsqrt(jnp.mean(xf * xf, axis=-1, keepdims=True) + EPS)
    return (y * w.astype(jnp.float32)).astype(x.dtype)


def modulate(x, shift, scale):
    return x * (1.0 + scale) + shift


def dwconv3(x, w):
    xp = jnp.pad(x, ((0, 0), (1, 1), (0, 0)))
    return xp[:, :-2] * w[0] + xp[:, 1:-1] * w[1] + xp[:, 2:] * w[2]


def axial_rope_angles(n_tokens, dim):
    t = jnp.arange(n_tokens)
    rows = (t // GRID_W).astype(jnp.float32)
    cols = (t % GRID_W).astype(jnp.float32)
    quarter = dim // 4
    freqs = ROPE_THETA ** (-(jnp.arange(quarter, dtype=jnp.float32) / quarter))
    return rows[:, None] * freqs[None], cols[:, None] * freqs[None]


def _rot_half(x, ang):
    cos = jnp.cos(ang).astype(x.dtype)
    sin = jnp.sin(ang).astype(x.dtype)
    x1, x2 = jnp.split(x, 2, axis=-1)
    return jnp.concatenate([x1 * cos - x2 * sin, x2 * cos + x1 * sin], axis=-1)


def apply_axial_rope(x, ang_row, ang_col):
    half = x.shape[-1] // 2
    return jnp.concatenate([_rot_half(x[..., :half], ang_row), _rot_half(x[..., half:], ang_col)], axis=-1)


def merge_heads(o):
    b, h, s, d = o.shape
    return o.transpose(0, 2, 1, 3).reshape(b, s, h * d)


def diff_attention(h_l, h_c, lam_q1, lam_k1, lam_q2, lam_k2, subln_w, layer_idx, ang_row, ang_col, need_ctx):
    f32 = jnp.float32
    lambda_init = 0.8 - 0.6 * math.exp(-0.3 * layer_idx)
    lam = (jnp.exp(jnp.sum(lam_q1.astype(f32) * lam_k1.astype(f32)))
           - jnp.exp(jnp.sum(lam_q2.astype(f32) * lam_k2.astype(f32))) + lambda_init)
    scale = DIFF_QK_DIM ** -0.5

    def qkv(h):
        b, s, _ = h.shape
        q = h[..., :A_Q].reshape(b, s, DIFF_HEADS, 2, DIFF_QK_DIM).transpose(0, 2, 3, 1, 4)
        k = h[..., A_Q:A_Q + A_K].reshape(b, s, DIFF_HEADS, 2, DIFF_QK_DIM).transpose(0, 2, 3, 1, 4)
        v = h[..., A_Q + A_K:].reshape(b, s, DIFF_HEADS, HEAD_DIM).transpose(0, 2, 1, 3)
        return q, k, v

    def attend(q, k, v):
        s = jnp.einsum('bhiqd,bhikd->bhiqk', q, k).astype(f32) * scale
        p = jax.nn.softmax(s, axis=-1)
        w = (p[:, :, 0] - lam * p[:, :, 1]).astype(v.dtype)
        o = jnp.einsum('bhqk,bhkd->bhqd', w, v)
        return rmsnorm(o, subln_w) * (1.0 - lambda_init)

    ql, kl, vl = qkv(h_l)
    qc, kc, vc = qkv(h_c)
    ql = apply_axial_rope(ql, ang_row, ang_col)
    kl = apply_axial_rope(kl, ang_row, ang_col)
    k_all = jnp.concatenate([kl, kc], axis=3)
    v_all = jnp.concatenate([vl, vc], axis=2)
    b, h, _, s, d = ql.shape
    nb = s // Q_BLOCK
    qb = ql.reshape(b, h, 2, nb, Q_BLOCK, d).transpose(3, 0, 1, 2, 4, 5)
    ob = lax.map(lambda q: attend(q, k_all, v_all), qb)
    o_l = merge_heads(ob.transpose(1, 2, 0, 3, 4).reshape(b, h, s, HEAD_DIM))
    o_c = merge_heads(attend(qc, kc, vc)) if need_ctx else None
    return o_l, o_c


def window_gqa(h_l, h_c, sink, ang_row, ang_col, need_ctx):
    f32 = jnp.float32
    scale = HEAD_DIM ** -0.5
    sink_f = sink.astype(f32).reshape(SWA_KV_HEADS, SWA_GROUP)

    def qkv(h):
        b, s, _ = h.shape
        q = h[..., :B_Q].reshape(b, s, SWA_KV_HEADS, SWA_GROUP, HEAD_DIM).transpose(0, 2, 3, 1, 4)
        k = h[..., B_Q:B_Q + B_KV].reshape(b, s, SWA_KV_HEADS, HEAD_DIM).transpose(0, 2, 1, 3)
        v = h[..., B_Q + B_KV:].reshape(b, s, SWA_KV_HEADS, HEAD_DIM).transpose(0, 2, 1, 3)
        return q, k, v

    ql, kl, vl = qkv(h_l)
    qc, kc, vc = qkv(h_c)
    ql = apply_axial_rope(ql, ang_row, ang_col)
    kl = apply_axial_rope(kl, ang_row, ang_col)
    b, _, _, s, d = ql.shape
    n_ctx = kc.shape[2]
    nb = s // SWA_BLOCK
    pad = ((0, 0), (0, 0), (SWA_BLOCK, SWA_BLOCK), (0, 0))
    kp = jnp.pad(kl, pad).reshape(b, SWA_KV_HEADS, nb + 2, SWA_BLOCK, d)
    vp = jnp.pad(vl, pad).reshape(b, SWA_KV_HEADS, nb + 2, SWA_BLOCK, d)
    kband = jnp.concatenate([kp[:, :, 0:nb], kp[:, :, 1:nb + 1], kp[:, :, 2:nb + 2]], axis=3)
    vband = jnp.concatenate([vp[:, :, 0:nb], vp[:, :, 1:nb + 1], vp[:, :, 2:nb + 2]], axis=3)
    qb = ql.reshape(b, SWA_KV_HEADS, SWA_GROUP, nb, SWA_BLOCK, d)
    s_loc = jnp.einsum('bkgnqd,bknjd->bkgnqj', qb, kband).astype(f32) * scale
    blk = jnp.arange(nb)[:, None, None]
    qpos = blk * SWA_BLOCK + jnp.arange(SWA_BLOCK)[None, :, None]
    kpos = blk * SWA_BLOCK - SWA_BLOCK + jnp.arange(3 * SWA_BLOCK)[None, None, :]
    band = (jnp.abs(kpos - qpos) <= SWA_WINDOW) & (kpos >= 0) & (kpos < s)
    s_loc = jnp.where(band, s_loc, NEG_INF)
    s_ctx = jnp.einsum('bkgnqd,bkld->bkgnql', qb, kc).astype(f32) * scale
    s_snk = jnp.broadcast_to(sink_f[None, :, :, None, None, None], s_loc.shape[:-1] + (1,))
    p = jax.nn.softmax(jnp.concatenate([s_loc, s_ctx, s_snk], axis=-1), axis=-1)
    nk = 3 * SWA_BLOCK
    o = (jnp.einsum('bkgnqj,bknjd->bkgnqd', p[..., :nk].astype(vl.dtype), vband)
         + jnp.einsum('bkgnql,bkld->bkgnqd', p[..., nk:nk + n_ctx].astype(vl.dtype), vc))
    o_l = merge_heads(o.reshape(b, SWA_HEADS, s, d))
    o_c = None
    if need_ctx:
        sc = jnp.einsum('bkgqd,bkld->bkgql', qc, kc).astype(f32) * scale
        sc_snk = jnp.broadcast_to(sink_f[None, :, :, None, None], sc.shape[:-1] + (1,))
        pc = jax.nn.softmax(jnp.concatenate([sc, sc_snk], axis=-1), axis=-1)
        oc = jnp.einsum('bkgql,bkld->bkgqd', pc[..., :n_ctx].astype(vc.dtype), vc)
        o_c = merge_heads(oc.reshape(b, SWA_HEADS, n_ctx, d))
    return o_l, o_c


def neighborhood_attention(h_l, h_c, rpb, need_ctx):
    f32 = jnp.float32
    scale = HEAD_DIM ** -0.5

    def qkv(h):
        b, s, _ = h.shape
        q = h[..., :C_W].reshape(b, s, NA_HEADS, HEAD_DIM).transpose(0, 2, 1, 3)
        k = h[..., C_W:2 * C_W].reshape(b, s, NA_HEADS, HEAD_DIM).transpose(0, 2, 1, 3)
        v = h[..., 2 * C_W:].reshape(b, s, NA_HEADS, HEAD_DIM).transpose(0, 2, 1, 3)
        return q, k, v

    ql, kl, vl = qkv(h_l)
    qc, kc, vc = qkv(h_c)
    b, h, s, d = ql.shape
    n_ctx = kc.shape[2]
    rows = s // GRID_W
    kr = min(NA_ROWS_MAX, rows)
    qg = ql.reshape(b, h, rows, GRID_W, d)
    kg = kl.reshape(b, h, rows, GRID_W, d)
    vg = vl.reshape(b, h, rows, GRID_W, d)
    r = jnp.arange(rows)
    row_start = jnp.clip(r - kr // 2, 0, rows - kr)
    key_rows = row_start[:, None] + jnp.arange(kr)[None, :]
    krows = jnp.take(kg, key_rows, axis=2)
    vrows = jnp.take(vg, key_rows, axis=2)
    s_loc = jnp.einsum('bhrqd,bhrjcd->bhrqjc', qg, krows).astype(f32) * scale
    cq = jnp.arange(GRID_W)
    col_start = jnp.clip(cq - NA_COLS // 2, 0, GRID_W - NA_COLS)
    ck = jnp.arange(GRID_W)
    col_mask = (ck[None, :] >= col_start[:, None]) & (ck[None, :] < col_start[:, None] + NA_COLS)
    dr = key_rows - r[:, None]
    dc = jnp.clip(ck[None, :] - cq[:, None], -(NA_COLS - 1), NA_COLS - 1)
    bias = rpb[:, (dr + NA_ROWS_MAX - 1)[:, None, :, None], (dc + NA_COLS - 1)[None, :, None, :]]
    s_loc = jnp.where(col_mask[None, None, None, :, None, :], s_loc + bias[None].astype(f32), NEG_INF)
    s_loc = s_loc.reshape(b, h, rows, GRID_W, kr * GRID_W)
    s_ctx = jnp.einsum('bhrqd,bhld->bhrql', qg, kc).astype(f32) * scale
    p = jax.nn.softmax(jnp.concatenate([s_loc, s_ctx], axis=-1), axis=-1)
    nk = kr * GRID_W
    p_loc = p[..., :nk].reshape(b, h, rows, GRID_W, kr, GRID_W).astype(vl.dtype)
    o = (jnp.einsum('bhrqjc,bhrjcd->bhrqd', p_loc, vrows)
         + jnp.einsum('bhrql,bhld->bhrqd', p[..., nk:].astype(vl.dtype), vc))
    o_l = merge_heads(o.reshape(b, h, s, d))
    o_c = None
    if need_ctx:
        sc = jnp.einsum('bhqd,bhkd->bhqk', qc, kc).astype(f32) * scale
        pc = jax.nn.softmax(sc, axis=-1).astype(vc.dtype)
        o_c = merge_heads(jnp.einsum('bhqk,bhkd->bhqd', pc, vc))
    return o_l, o_c


def short_conv_mixer(h_l, h_c, conv_w, need_ctx):
    def run(h):
        bg = h[..., :SCONV_CH]
        cg = h[..., SCONV_CH:2 * SCONV_CH]
        xi = h[..., 2 * SCONV_CH:]
        return bg * dwconv3(cg * xi, conv_w)
    o_l = run(h_l)
    o_c = run(h_c) if need_ctx else None
    return o_l, o_c


def conv_ffn(h, w_up, conv_w, conv_b, w_down):
    u = dwconv3(h @ w_up, conv_w) + conv_b
    g, v = jnp.split(u, 2, axis=-1)
    return (jax.nn.silu(g) * v) @ w_down


def setup_inputs(seed: int = 0) -> dict:
    key = jax.random.key(seed)
    ks = jax.random.split(key, 24)
    f32 = jnp.float32
    nrm = lambda k, shape, s: jax.random.normal(k, shape, f32) * s
    return {
        "x": nrm(ks[0], (BATCH, SEQ, D_MODEL), 1.0),
        "c": nrm(ks[1], (BATCH, D_MODEL), 1.0),
        "ctx": nrm(ks[2], (BATCH, CTX_LEN, D_MODEL), 1.0),
        "c_ctx": nrm(ks[3], (D_MODEL,), 1.0),
        "norm_mix_w": 1.0 + nrm(ks[4], (DEPTH, D_MODEL), 0.01),
        "norm_ffn_w": 1.0 + nrm(ks[5], (DEPTH, D_MODEL), 0.01),
        "w_mod": nrm(ks[6], (DEPTH, D_MODEL, 6 * D_MODEL), 0.5 * D_MODEL ** -0.5),
        "b_mod": nrm(ks[7], (DEPTH, 6 * D_MODEL), 0.02),
        "w_in": nrm(ks[8], (DEPTH, D_MODEL, IN_COLS), D_MODEL ** -0.5),
        "w_out": nrm(ks[9], (DEPTH, MIX_WIDTH, D_MODEL), MIX_WIDTH ** -0.5),
        "diff_lambda_q1": nrm(ks[10], (DEPTH, DIFF_QK_DIM), 0.1),
        "diff_lambda_k1": nrm(ks[11], (DEPTH, DIFF_QK_DIM), 0.1),
        "diff_lambda_q2": nrm(ks[12], (DEPTH, DIFF_QK_DIM), 0.1),
        "diff_lambda_k2": nrm(ks[13], (DEPTH, DIFF_QK_DIM), 0.1),
        "diff_subln_w": 1.0 + nrm(ks[14], (DEPTH, HEAD_DIM), 0.01),
        "swa_sink": nrm(ks[15], (DEPTH, SWA_HEADS), 0.5),
        "na_rpb": nrm(ks[16], (DEPTH, NA_HEADS, 2 * NA_ROWS_MAX - 1, 2 * NA_COLS - 1), 0.02),
        "sconv_w": nrm(ks[17], (DEPTH, SCONV_WIDTH, SCONV_CH), SCONV_WIDTH ** -0.5),
        "ffn_w_up": nrm(ks[18], (DEPTH, D_MODEL, 2 * D_FF), D_MODEL ** -0.5),
        "ffn_conv_w": nrm(ks[19], (DEPTH, FFN_CONV_WIDTH, 2 * D_FF), FFN_CONV_WIDTH ** -0.5),
        "ffn_conv_b": nrm(ks[20], (DEPTH, 2 * D_FF), 0.02),
        "ffn_w_down": nrm(ks[21], (DEPTH, D_FF, D_MODEL), D_FF ** -0.5),
        "final_norm_w": 1.0 + nrm(ks[22], (D_MODEL,), 0.01),
    }


def reference(x, c, ctx, c_ctx, norm_mix_w, norm_ffn_w, w_mod, b_mod, w_in, w_out,
              diff_lambda_q1, diff_lambda_k1, diff_lambda_q2, diff_lambda_k2, diff_subln_w,
              swa_sink, na_rpb, sconv_w, ffn_w_up, ffn_conv_w, ffn_conv_b, ffn_w_down, final_norm_w):
    s = x.shape[1]
    ang_a_row, ang_a_col = axial_rope_angles(s, DIFF_QK_DIM)
    ang_b_row, ang_b_col = axial_rope_angles(s, HEAD_DIM)
    c_act = jax.nn.silu(c)
    cc_act = jax.nn.silu(c_ctx)
    o1 = A_COLS
    o2 = o1 + B_COLS
    o3 = o2 + C_COLS
    xl, xc = x, ctx
    for l in range(DEPTH):
        need_ctx = l < DEPTH - 1
        mod_l = (c_act @ w_mod[l] + b_mod[l])[:, None, :]
        mod_c = (cc_act @ w_mod[l] + b_mod[l])[None, None, :]
        sh1, sc1, g1, sh2, sc2, g2 = jnp.split(mod_l, 6, axis=-1)
        csh1, csc1, cg1, csh2, csc2, cg2 = jnp.split(mod_c, 6, axis=-1)
        hl = modulate(rmsnorm(xl, norm_mix_w[l]), sh1, sc1) @ w_in[l]
        hc = modulate(rmsnorm(xc, norm_mix_w[l]), csh1, csc1) @ w_in[l]
        oa_l, oa_c = diff_attention(hl[..., :o1], hc[..., :o1], diff_lambda_q1[l], diff_lambda_k1[l],
                                    diff_lambda_q2[l], diff_lambda_k2[l], diff_subln_w[l], l,
                                    ang_a_row, ang_a_col, need_ctx)
        ob_l, ob_c = window_gqa(hl[..., o1:o2], hc[..., o1:o2], swa_sink[l], ang_b_row, ang_b_col, need_ctx)
        oc_l, oc_c = neighborhood_attention(hl[..., o2:o3], hc[..., o2:o3], na_rpb[l], need_ctx)
        od_l, od_c = short_conv_mixer(hl[..., o3:], hc[..., o3:], sconv_w[l], need_ctx)
        yl = jnp.concatenate([oa_l, ob_l, oc_l, od_l], axis=-1) @ w_out[l]
        xl = xl + g1 * yl
        fl = modulate(rmsnorm(xl, norm_ffn_w[l]), sh2, sc2)
        xl = xl + g2 * conv_ffn(fl, ffn_w_up[l], ffn_conv_w[l], ffn_conv_b[l], ffn_w_down[l])
        if need_ctx:
            yc = jnp.concatenate([oa_c, ob_c, oc_c, od_c], axis=-1) @ w_out[l]
            xc = xc + cg1 * yc
            fc = modulate(rmsnorm(xc, norm_ffn_w[l]), csh2, csc2)
            xc = xc + cg2 * conv_ffn(fc, ffn_w_up[l], ffn_conv_w[l], ffn_conv_b[l], ffn_w_down[l])
    return rmsnorm(xl, final_norm_w)
```

```python
import math
import numpy as np
import ml_dtypes
import concourse.bass as bass
import concourse.mybir as mybir
from concourse.alu_op_type import AluOpType as ALU
from concourse.bass_utils import run_bass_kernel_spmd

F32 = mybir.dt.float32
BF16 = mybir.dt.bfloat16
AF = mybir.ActivationFunctionType
AX = mybir.AxisListType

D = 1024
CTX = 256
DFF = 2816
EPS = 1e-6


class Buf:
    __slots__ = ("w", "r", "excl")

    def __init__(self, excl=False):
        self.w = None
        self.r = {}
        self.excl = excl


class Sched:
    NDS = 32

    def __init__(self, nc):
        self.nc = nc
        self.e = dict(pe=nc.tensor, act=nc.scalar, dve=nc.vector, pool=nc.gpsimd, sp=nc.sync)
        self.sem = {k: nc.alloc_semaphore(name=f"c_{k}") for k in ("pe", "act", "dve", "pool")}
        self.cnt = {k: 0 for k in self.sem}
        self.seen = {k: {} for k in self.e}
        self.dsem = [nc.alloc_semaphore(name=f"d_{i}") for i in range(self.NDS)]
        self.dcnt = [0] * self.NDS
        self.dnext = 0
        self.pnext = 0
        self.n = 0

    def _wait(self, eng, key, val):
        if self.seen[eng].get(key, 0) >= val:
            return
        sem = self.sem[key[1]] if key[0] == "c" else self.dsem[key[1]]
        self.e[eng].wait_ge(sem, val)
        self.seen[eng][key] = val
        self.n += 1

    def _dep1(self, eng, key, val):
        if key[0] == "c" and key[1] == eng and eng == "pe":
            return
        self._wait(eng, key, val)

    def _deps(self, eng, reads, writes):
        for b in reads:
            if b.w is not None:
                self._dep1(eng, *b.w)
            if b.excl:
                for key, val in b.r.items():
                    self._dep1(eng, key, val)
        for b in writes:
            if b.w is not None:
                self._dep1(eng, *b.w)
            for key, val in b.r.items():
                self._dep1(eng, key, val)

    def _mark(self, tok, reads, writes):
        key, val = tok
        for b in reads:
            if b.r.get(key, 0) < val:
                b.r[key] = val
        for b in writes:
            b.w = tok
            b.r = {}

    def op(self, eng, fn, reads=(), writes=(), inc=True):
        self._deps(eng, reads, writes)
        ins = fn(self.e[eng])
        self.n += 1
        if inc:
            self.cnt[eng] += 1
            ins.then_inc(self.sem[eng], 1)
            tok = (("c", eng), self.cnt[eng])
        else:
            tok = (("c", eng), self.cnt[eng] + 1)
        self._mark(tok, reads, writes)

    def dma(self, q, out, in_, reads=(), writes=()):
        self._deps(q, reads, writes)
        if q == "pool":
            i = 24 + self.pnext
            self.pnext = (self.pnext + 1) % 8
        else:
            i = self.dnext
            self.dnext = (i + 1) % 24
        if self.dcnt[i] > 0:
            self._wait(q, ("d", i), 16 * self.dcnt[i])
        ins = self.e[q].dma_start(out=out, in_=in_)
        self.n += 1
        self.dcnt[i] += 1
        ins.then_inc(self.dsem[i], 16)
        self._mark((("d", i), 16 * self.dcnt[i]), reads, writes)

    def barrier(self):
        for eng in self.e:
            for k in self.sem:
                if k != eng and self.cnt[k] > 0:
                    self._wait(eng, ("c", k), self.cnt[k])
            for i in range(self.NDS):
                if self.dcnt[i] > 0:
                    self._wait(eng, ("d", i), 16 * self.dcnt[i])

    def finish(self, bufs, eng="sp"):
        for b in bufs:
            if b.w is not None:
                self._wait(eng, *b.w)


class Ring:
    def __init__(self, alloc, shape, dtype, n):
        tb = [alloc(shape, dtype) for i in range(n)]
        self.t = [t for t, b in tb]
        self.b = [b for t, b in tb]
        self.i = 0

    def next(self):
        k = self.i
        self.i = (k + 1) % len(self.t)
        return self.t[k], self.b[k]


def fm_plan():
    blocks = []
    o1, o2, o3 = 768, 1280, 2048

    def partner(dim):
        q = dim // 4
        idx = np.arange(dim)
        p = np.where((idx % (2 * q)) < q, idx + q, idx - q)
        return p

    pa = partner(32)
    for base in (0, 256):
        for h in range(4):
            main = base + h * 64 + np.arange(64)
            part = base + h * 64 + np.concatenate([pa, 32 + pa])
            blocks.append((64, main, part, "ropeA"))
    pb = partner(64)
    for g in range(2):
        main = np.concatenate([o1 + (k * 2 + g) * 64 + np.arange(64) for k in range(2)])
        part = np.concatenate([o1 + (k * 2 + g) * 64 + pb for k in range(2)])
        blocks.append((128, main, part, "ropeB"))
    main = o1 + 256 + np.arange(128)
    part = np.concatenate([o1 + 256 + k * 64 + pb for k in range(2)])
    blocks.append((128, main, part, "ropeB"))
    for base in (o2, o2 + 256):
        for j in range(2):
            blocks.append((128, base + j * 128 + np.arange(128), None, "plain"))
    for j in range(2):
        blocks.append((128, o3 + j * 128 + np.arange(128), None, "plain"))
    for j in range(2):
        blocks.append((128, o3 + 256 + j * 128 + np.arange(128), o3 + 512 + j * 128 + np.arange(128), "mul"))
    return blocks


BLK_QA, BLK_KA, BLK_QB, BLK_KB, BLK_QC, BLK_KC, BLK_BG, BLK_M = 0, 4, 8, 10, 11, 13, 15, 17
NBLK = 19
VCOLS = np.concatenate([512 + np.arange(256), 768 + 384 + np.arange(128), 1280 + 512 + np.arange(256)])


def rope_tables(S):
    t = np.arange(S)
    rows = (t // 64).astype(np.float32)
    cols = (t % 64).astype(np.float32)
    out = []
    for dim in (32, 64):
        q = dim // 4
        freqs = (10000.0 ** (-(np.arange(q, dtype=np.float32) / q))).astype(np.float32)
        d = np.arange(128) % dim
        pos = np.where((d < 2 * q)[:, None], rows[None, :], cols[None, :])
        ang = pos * freqs[d % q][:, None]
        sign = np.where((d % (2 * q)) < q, -1.0, 1.0)[:, None]
        out.append(np.cos(ang).astype(np.float32))
        out.append((np.sin(ang) * sign).astype(np.float32))
    return out


def c_validity(S):
    rows = S // 64
    rs = np.clip(np.arange(rows) - 4, 0, rows - 8)
    return rs


def build(S, DEPTH):
    T = S + CTX
    NT = T // 128
    NGL = S // 512
    ROWS = S // 64
    plan = fm_plan()
    W1C = sum(r * (2 if p is not None else 1) for r, m, p, k in plan) + 640
    nc = bass.Bass("TRN2", target_bir_lowering=False)

    def din(name, shape, dt=F32):
        return nc.dram_tensor(name, list(shape), dt, kind="ExternalInput").ap()

    x_in = din("x", [S, D])
    ctx_in = din("ctx", [CTX, D])
    cvec = din("cvec", [128, 8, 2])
    nmw = din("nmw", [DEPTH, 128, 8])
    nfw = din("nfw", [DEPTH, 128, 8])
    fnw = din("fnw", [128, D])
    wmod = din("wmod", [DEPTH, D, 6 * D])
    bmod = din("bmod", [DEPTH, 128, 48])
    w1d = din("w1", [DEPTH, D, W1C])
    woutd = din("wout", [DEPTH, D, D])
    lamd = din("lam", [DEPTH, 128, 4, 32])
    sublnd = din("subln", [DEPTH, 128, 64])
    sinkd = din("sink", [DEPTH, 128, 4])
    rpbd = din("rpb", [DEPTH, 4, 15, 128])
    sconvd = din("sconv", [DEPTH, 128, 2, 3])
    wupd = din("wup", [DEPTH, D, 2 * DFF])
    fcwd = din("fcw", [DEPTH, 128, 44, 3])
    fcbd = din("fcb", [DEPTH, 128, 44])
    wdnd = din("wdn", [DEPTH, DFF, D])
    identd = din("ident", [128, 128])
    ropd = [din(f"rope{i}", [128, S]) for i in range(4)]
    maskBd = din("maskB", [6, 128, 512], BF16)
    colmd = din("colmask", [128, 64])
    out_d = nc.dram_tensor("out", [S, D], F32, kind="ExternalOutput").ap()

    xres = nc.dram_tensor("xres", [T, D], F32).ap()
    xmid = nc.dram_tensor("xmid", [T, D], F32).ap()
    FM = nc.dram_tensor("fm", [NBLK * 128, T], BF16).ap()
    Vs = nc.dram_tensor("vs", [T, 640], BF16).ap()
    Ed = nc.dram_tensor("etab", [3, 4, 128, 8, 512], BF16).ap()
    b_xres, b_xmid, b_FM, b_Vs, b_Ed, b_out = Buf(), Buf(), Buf(), Buf(), Buf(), Buf()

    with nc.Block():
        S_ = Sched(nc)
        op, dma = S_.op, S_.dma
        _n = [0]

        def sb(shape, dt=F32):
            _n[0] += 1
            return nc.alloc_sbuf_tensor(f"t{_n[0]}", list(shape), dt), Buf()

        PS = [nc.alloc_psum_tensor(f"ps{i}", [128, 2, 512], F32) for i in range(4)]
        PB = [Buf(excl=True) for _ in range(4)]

        ident, b_id = sb([128, 128])
        identb, b_idb = sb([128, 128], BF16)
        ones_f, b_ones = sb([128, 128])
        epsT, b_eps = sb([128, 1])
        dma("sp", ident[:], identd[:, :], writes=[b_id])
        op("dve", lambda e: e.tensor_copy(out=identb[:], in_=ident[:]), reads=[b_id], writes=[b_idb])
        op("dve", lambda e: e.memset(ones_f[:], 1.0), writes=[b_ones])
        op("dve", lambda e: e.memset(epsT[:], EPS), writes=[b_eps])
        colm, b_colm = sb([128, 64])
        dma("sp", colm[:], colmd[:, :], writes=[b_colm])

        dma("sp", xres[0:S, :], x_in[:, :], writes=[b_xres])
        dma("sp", xres[S:T, :], ctx_in[:, :], writes=[b_xres])

        cv, b_cv = sb([128, 8, 2])
        cact, b_cact = sb([128, 8, 2], BF16)
        dma("sp", cv[:], cvec[:, :, :], writes=[b_cv])
        op("act", lambda e: e.activation(out=cact[:], in_=cv[:], func=AF.Silu), reads=[b_cv], writes=[b_cact])

        wbig, b_wbig = sb([128, 8, 5632], BF16)
        wdn, b_wdn = sb([128, 22, D], BF16)
        wflat = wbig[:].rearrange("p c n -> p (c n)")
        dflat = wdn[:].rearrange("p c n -> p (c n)")
        kA = wflat[:, 0:4 * T].rearrange("p (h t) -> p h t", h=4); b_kA = b_wbig
        wout = wflat[:, 4 * T:4 * T + 8192].rearrange("p (c n) -> p c n", c=8); b_wout = b_wbig
        maskB = wflat[:, 4 * T + 8192:4 * T + 8192 + 3072].rearrange("p (r q) -> p r q", r=6); b_maskB = b_wbig
        vA = dflat[:, 0:NT * 260].rearrange("p (j h d) -> p j h d", h=4, d=65); b_vA = b_wdn
        g1off = ((NT * 260 + 15) // 16) * 16
        Gb1 = [dflat[:, g1off + ty * 2048:g1off + (ty + 1) * 2048].bitcast(F32) for ty in range(2)]
        assert 4 * T + 8192 + 3072 <= 8 * 5632 and g1off + 4096 <= 22 * D
        modT, b_modT = sb([128, 48, 2])
        bmT, b_bmT = sb([128, 48])
        nmT, b_nmT = sb([128, 8])
        nfT, b_nfT = sb([128, 8])
        A1, b_A1 = sb([128, 8, 2]); A2, b_A2 = sb([128, 8, 2])
        lamT, b_lam = sb([128, 4, 32]); lamw, b_lamw = sb([128, 2, 32]); lams, b_lams = sb([128, 2])
        neglam, b_neglam = sb([128, 1])
        subw, b_subw = sb([128, 64])
        esink, b_esink = sb([128, 4])
        scw, b_scw = sb([128, 2, 3])
        fcw, b_fcw = sb([128, 44, 3]); fcb, b_fcb = sb([128, 44])
        ss_r = Ring(sb, [128, 1], F32, 4)
        rstd_r = Ring(sb, [128, 1], F32, 4)
        sm_r = Ring(sb, [128, 8], F32, 6)
        NA = (nc.sbuf_bytes_remaining - 1024) // 64 * 32
        arena_t = nc.alloc_sbuf_tensor("arena", [128, NA], BF16)
        aoff = [0]

        def ar(shape, dt=F32):
            ne = int(np.prod(shape[1:]))
            nb = ((ne * (4 if dt == F32 else 2) + 31) // 32) * 32
            o = aoff[0]
            aoff[0] += nb // 2
            assert aoff[0] <= NA, (aoff[0], NA)
            v = arena_t[:, o:o + nb // 2]
            if dt == F32:
                v = v.bitcast(F32)
            v = v[:, 0:ne]
            if len(shape) == 3:
                v = v.rearrange("p (a b) -> p a b", a=shape[1])
            elif len(shape) == 4:
                v = v.rearrange("p (a b c) -> p a b c", a=shape[1], b=shape[2])
            return v, Buf()

        def phase():
            S_.barrier()
            aoff[0] = 0

        groups = [(g * 512, 512, 0) for g in range(NGL)] + [(S, 256, 1)]

        def rmsnorm_T(src_dram, b_src, t0, ntile, Aw, b_Aw, dst, b_dst, col0):
            for i in range(ntile):
                xt, bxt = xring.next()
                dma("sp", xt[:], src_dram[t0 + 128 * i:t0 + 128 * (i + 1), :], reads=[b_src], writes=[bxt])
                ss, bss = ss_r.next(); rs, brs = rstd_r.next(); xn, bxn = xnring.next()
                op("act", lambda e: e.activation(out=sq[:], in_=xt[:], func=AF.Square, accum_out=ss[:]), reads=[bxt], writes=[b_sq, bss])
                op("act", lambda e: e.activation(out=rs[:], in_=ss[:], func=AF.Sqrt, bias=epsT[:], scale=1.0 / D), reads=[bss, b_eps], writes=[brs])
                op("dve", lambda e: e.reciprocal(out=rs[:], in_=rs[:]), reads=[brs], writes=[brs])
                op("act", lambda e: e.activation(out=xn[:], in_=xt[:], func=AF.Copy, scale=rs[:]), reads=[bxt, brs], writes=[bxn])
                pt = PS[3][:].bitcast(BF16)
                for c in range(8):
                    op("pe", lambda e: e.transpose(out=pt[:, 0, c * 128:(c + 1) * 128], in_=xn[:, c * 128:(c + 1) * 128], identity=identb[:]),
                       reads=[bxn, b_idb], writes=[PB[3]], inc=(c == 7))
                for c in range(8):
                    op("dve", lambda e: e.tensor_scalar(out=dst[:, c, col0 + 128 * i:col0 + 128 * (i + 1)], in0=pt[:, 0, c * 128:(c + 1) * 128],
                                                        scalar1=Aw[0][:, c, Aw[2]:Aw[2] + 1], scalar2=Aw[1][:, c, Aw[2]:Aw[2] + 1], op0=ALU.mult, op1=ALU.add),
                       reads=[PB[3], b_Aw], writes=[b_dst])

        for l in range(DEPTH):
            need_ctx = l < DEPTH - 1
            lam_init = 0.8 - 0.6 * math.exp(-0.3 * l)
            phase()
            eblk, b_eblk = ar([128, 4, 15, 64])
            eblk2, b_eblk2 = ar([128, 4, 15, 64])
            eblkb, b_eblkb = ar([128, 4, 15, 64], BF16)
            etile, b_etile = ar([128, 8, 512], BF16)
            dma("sp", bmT[:], bmod[l], writes=[b_bmT])
            dma("sp", nmT[:], nmw[l], writes=[b_nmT])
            dma("sp", nfT[:], nfw[l], writes=[b_nfT])
            dma("sp", lamT[:], lamd[l], writes=[b_lam])
            dma("sp", subw[:], sublnd[l], writes=[b_subw])
            dma("sp", esink[:], sinkd[l], writes=[b_esink])
            dma("sp", scw[:], sconvd[l], writes=[b_scw])
            dma("sp", fcw[:], fcwd[l], writes=[b_fcw])
            dma("sp", fcb[:], fcbd[l], writes=[b_fcb])
            for piece in range(6):
                dma("pool", wbig[:, :, 0:1024], wmod[l][:, piece * 1024:(piece + 1) * 1024].rearrange("(c p) n -> p c n", p=128), writes=[b_wbig])
                for jj in range(8):
                    j = piece * 8 + jj
                    for c in range(8):
                        op("pe", lambda e: e.matmul(PS[0][:, 0, 2 * j:2 * j + 2], lhsT=wbig[:, c, jj * 128:(jj + 1) * 128], rhs=cact[:, c, :], start=(c == 0), stop=(c == 7)),
                           reads=[b_wbig, b_cact], writes=[PB[0]], inc=(c == 7 and jj == 7))
            op("dve", lambda e: e.tensor_tensor(out=modT[:], in0=PS[0][:, 0, 0:96].rearrange("p (j t) -> p j t", t=2), in1=bmT[:].unsqueeze(2).to_broadcast([128, 48, 2]), op=ALU.add),
               reads=[PB[0], b_bmT], writes=[b_modT])
            for (Aw, bAw, nw, bnw, v) in ((A1, b_A1, nmT, b_nmT, 1), (A2, b_A2, nfT, b_nfT, 4)):
                op("dve", lambda e: e.tensor_scalar(out=Aw[:], in0=modT[:, v * 8:(v + 1) * 8, :], scalar1=1.0, scalar2=None, op0=ALU.add), reads=[b_modT], writes=[bAw])
                op("dve", lambda e: e.tensor_tensor(out=Aw[:], in0=Aw[:], in1=nw[:].unsqueeze(2).to_broadcast([128, 8, 2]), op=ALU.mult), reads=[bAw, bnw], writes=[bAw])
            def make_gates(v, dsts):
                diag_r = Ring(ar, [128, 128], F32, 2)
                for ty in range(2):
                    for c in range(8):
                        dg, bdg = diag_r.next()
                        op("dve", lambda e: e.tensor_scalar(out=dg[:], in0=ident[:], scalar1=modT[:, v * 8 + c, ty:ty + 1], scalar2=None, op0=ALU.mult), reads=[b_id, b_modT], writes=[bdg])
                        op("pe", lambda e: e.matmul(PS[1][:, c // 4, (c % 4) * 128:(c % 4 + 1) * 128], lhsT=ones_f[:], rhs=dg[:], start=True, stop=True), reads=[b_ones, bdg], writes=[PB[1]])
                    gt, bgt = dsts[ty]
                    op("act", lambda e: e.activation(out=gt[:], in_=PS[1][:].rearrange("p a b -> p (a b)"), func=AF.Copy), reads=[PB[1]], writes=[bgt])

            op("dve", lambda e: e.tensor_tensor(out=lamw[:], in0=lamT[:, 0:4:2, :], in1=lamT[:, 1:4:2, :], op=ALU.mult), reads=[b_lam], writes=[b_lamw])
            op("dve", lambda e: e.tensor_reduce(out=lams[:], in_=lamw[:], axis=AX.X, op=ALU.add), reads=[b_lamw], writes=[b_lams])
            op("act", lambda e: e.activation(out=lams[:], in_=lams[:], func=AF.Exp), reads=[b_lams], writes=[b_lams])
            op("dve", lambda e: e.tensor_tensor(out=neglam[:], in0=lams[:, 1:2], in1=lams[:, 0:1], op=ALU.subtract), reads=[b_lams], writes=[b_neglam])
            op("dve", lambda e: e.tensor_scalar(out=neglam[:], in0=neglam[:], scalar1=-lam_init, scalar2=None, op0=ALU.add), reads=[b_neglam], writes=[b_neglam])
            op("dve", lambda e: e.tensor_scalar(out=subw[:], in0=subw[:], scalar1=1.0 - lam_init, scalar2=None, op0=ALU.mult), reads=[b_subw], writes=[b_subw])
            op("act", lambda e: e.activation(out=esink[:], in_=esink[:], func=AF.Exp), reads=[b_esink], writes=[b_esink])
            for h in range(4):
                for half in range(2):
                    src = bass.AP(rpbd.tensor, rpbd.offset + ((l * 4 + h) * 15) * 128, [[1, 64], [128, 15], [1, 64]])
                    dma("sp", eblk[half * 64:(half + 1) * 64, h, :, :], src, writes=[b_eblk])
            erev = bass.AP(eblk.tensor, eblk.offset + 63, [list(a) for a in eblk.ap[:-1]] + [[-1, 64]])
            op("act", lambda e: e.activation(out=eblk2[:], in_=erev, func=AF.Exp), reads=[b_eblk], writes=[b_eblk2])
            op("dve", lambda e: e.tensor_tensor(out=eblkb[:], in0=eblk2[:], in1=colm[:].unsqueeze(1).unsqueeze(1).to_broadcast([128, 4, 15, 64]), op=ALU.mult),
               reads=[b_eblk2, b_colm], writes=[b_eblkb])
            rsv = c_validity(S)
            for cls in range(3):
                if NGL < 3 and cls == 0:
                    continue
                g = {0: 1, 1: 0, 2: NGL - 1}[cls]
                for h in range(4):
                    op("pool", lambda e: e.memset(etile[:], 0.0), writes=[b_etile])
                    for rel in range(8):
                        jt = 4 * g - 2 + rel
                        if jt < 0 or jt >= S // 128:
                            continue
                        for a in range(2):
                            kr = 2 * jt + a
                            for b in range(8):
                                r = 8 * g + b
                                if not (rsv[r] <= kr < rsv[r] + 8):
                                    continue
                                dr = kr - r
                                op("pool", lambda e: e.tensor_copy(out=etile[a * 64:(a + 1) * 64, rel, b * 64:(b + 1) * 64], in_=eblkb[a * 64:(a + 1) * 64, h, dr + 7, :]),
                                   reads=[b_eblkb], writes=[b_etile])
                    dma("pool", Ed[cls, h], etile[:], reads=[b_etile], writes=[b_Ed])

            phase()
            xring = Ring(ar, [128, D], F32, 2)
            xnring = Ring(ar, [128, D], BF16, 2)
            sq, b_sq = ar([128, D])
            xh_r = Ring(ar, [128, 8, 512], BF16, 2)
            stg_r = Ring(ar, [128, 512], BF16, 4)
            f2_r = Ring(ar, [128, 512], F32, 4)
            vstg_r = Ring(ar, [128, 640], BF16, 2)
            ropeT = [ar([128, 512]) for _ in range(4)]
            dma("pool", wbig[:, :, 0:W1C], w1d[l].rearrange("(c p) n -> p c n", p=128), writes=[b_wbig])
            xh_next = None
            for gidx, (t0, n, ty) in enumerate(groups):
                if xh_next is None:
                    xh, bxh = xh_r.next()
                    rmsnorm_T(xres, b_xres, t0, n // 128, (A1, modT[:, 0:8, :], ty), b_A1, xh, bxh, 0)
                else:
                    xh, bxh = xh_next
                xh_next = None
                if gidx + 1 < len(groups):
                    t0n, nn, tyn = groups[gidx + 1]
                    xh_next = xh_r.next()
                    rmsnorm_T(xres, b_xres, t0n, nn // 128, (A1, modT[:, 0:8, :], tyn), b_A1, xh_next[0], xh_next[1], 0)
                if ty == 0:
                    for i in range(4):
                        dma("sp", ropeT[i][0][:], ropd[i][:, t0:t0 + 512], writes=[ropeT[i][1]])
                off = 0
                for bi, (rows, m, p, kind) in enumerate(plan):
                    pair = p is not None and not (kind.startswith("rope") and ty == 1)
                    pb = bi % 2
                    for c in range(8):
                        op("pe", lambda e: e.matmul(PS[pb][:rows, 0, :n], lhsT=wbig[:, c, off:off + rows], rhs=xh[:, c, :n], start=(c == 0), stop=(c == 7)),
                           reads=[b_wbig, bxh], writes=[PB[pb]], inc=(c == 7 and not pair))
                    if pair:
                        for c in range(8):
                            op("pe", lambda e: e.matmul(PS[pb][:rows, 1, :n], lhsT=wbig[:, c, off + rows:off + 2 * rows], rhs=xh[:, c, :n], start=(c == 0), stop=(c == 7)),
                               reads=[b_wbig, bxh], writes=[PB[pb]], inc=(c == 7))
                    stg, bstg = stg_r.next()
                    if not pair:
                        op("act", lambda e: e.activation(out=stg[:rows, :n], in_=PS[pb][:rows, 0, :n], func=AF.Copy), reads=[PB[pb]], writes=[bstg])
                    elif kind == "mul":
                        f, bf = f2_r.next()
                        op("act", lambda e: e.activation(out=f[:rows, :n], in_=PS[pb][:rows, 1, :n], func=AF.Copy), reads=[PB[pb]], writes=[bf])
                        op("dve", lambda e: e.tensor_tensor(out=stg[:rows, :n], in0=PS[pb][:rows, 0, :n], in1=f[:rows, :n], op=ALU.mult), reads=[PB[pb], bf], writes=[bstg])
                    else:
                        ct, bct = ropeT[0 if kind == "ropeA" else 2]
                        st, bst = ropeT[1 if kind == "ropeA" else 3]
                        f, bf = f2_r.next(); f2, bf2 = f2_r.next()
                        op("dve", lambda e: e.tensor_tensor(out=f[:rows, :n], in0=PS[pb][:rows, 0, :n], in1=ct[:rows, :n], op=ALU.mult), reads=[PB[pb], bct], writes=[bf])
                        op("dve", lambda e: e.tensor_tensor(out=f2[:rows, :n], in0=PS[pb][:rows, 1, :n], in1=st[:rows, :n], op=ALU.mult), reads=[PB[pb], bst], writes=[bf2])
                        op("pool", lambda e: e.tensor_tensor(out=stg[:rows, :n], in0=f[:rows, :n], in1=f2[:rows, :n], op=ALU.add), reads=[bf, bf2], writes=[bstg])
                    dma("sp", FM[bi * 128:bi * 128 + rows, t0:t0 + n], stg[:rows, :n], reads=[bstg], writes=[b_FM])
                    off += rows * (2 if p is not None else 1)
                for i in range(n // 128):
                    for c in range(8):
                        op("pe", lambda e: e.matmul(PS[2][:, 0, :], lhsT=xh[:, c, i * 128:(i + 1) * 128], rhs=wbig[:, c, off:off + 512], start=(c == 0), stop=(c == 7)),
                           reads=[b_wbig, bxh], writes=[PB[2]], inc=False)
                    for c in range(8):
                        op("pe", lambda e: e.matmul(PS[2][:, 1, 0:128], lhsT=xh[:, c, i * 128:(i + 1) * 128], rhs=wbig[:, c, off + 512:off + 640], start=(c == 0), stop=(c == 7)),
                           reads=[b_wbig, bxh], writes=[PB[2]], inc=(c == 7))
                    vs, bvs = vstg_r.next()
                    op("act", lambda e: e.activation(out=vs[:, 0:512], in_=PS[2][:, 0, :], func=AF.Copy), reads=[PB[2]], writes=[bvs])
                    op("act", lambda e: e.activation(out=vs[:, 512:640], in_=PS[2][:, 1, 0:128], func=AF.Copy), reads=[PB[2]], writes=[bvs])
                    dma("sp", Vs[t0 + i * 128:t0 + (i + 1) * 128, :], vs[:], reads=[bvs], writes=[b_Vs])

            phase()
            xring = Ring(ar, [128, D], F32, 1)
            sq, b_sq = ar([128, D])
            accs = sq.rearrange("p (a b) -> p a b", a=2); b_accs = b_sq
            kB, b_kB = ar([128, 1, 8 * 128], BF16)
            vB, b_vB = ar([128, 8, 2, 65], BF16)
            kC, b_kC = ar([128, 2, 10 * 128], BF16)
            vC, b_vC = ar([128, 10, 4, 65], BF16)
            qT, b_qT = ar([128, 8, 512], BF16)
            Eg, b_Eg = ar([128, 8, 512], BF16)
            pT_r = Ring(ar, [128, 2, 512], BF16, 2)
            Otok, b_Otok = ar([128, 4, 768], BF16)
            OT, b_OT = ar([128, 8, 512], BF16)
            o_r = Ring(ar, [128, 4, 64], F32, 2)
            bgw, b_bgw = ar([128, 2, 512], BF16)
            mw, b_mw = ar([128, 2, 514], BF16)
            f2_r = Ring(ar, [128, 512], F32, 1)
            op("pool", lambda e: e.memset(dflat[:, 0:NT * 260], 1.0), writes=[b_wdn])
            op("pool", lambda e: e.memset(vB[:], 1.0), writes=[b_vB])
            op("pool", lambda e: e.memset(vC[:], 1.0), writes=[b_vC])
            make_gates(2, [(Gb1[0], b_wdn), (Gb1[1], b_wdn)])
            Gb = [[(Gb1[0], b_wdn), (Gb1[1], b_wdn)], None]
            dma("sp", maskB[:], maskBd.rearrange("r p q -> p r q"), writes=[b_maskB])
            dma("pool", wout[:], woutd[l].rearrange("(c p) n -> p c n", p=128), writes=[b_wout])
            for h in range(4):
                dma("sp", kA[0:64, h, :], FM[(BLK_KA + h) * 128:(BLK_KA + h) * 128 + 64, :], reads=[b_FM], writes=[b_kA])
            for j in range(NT):
                dma("sp", vA[:, j, :, 0:64], Vs[j * 128:(j + 1) * 128, 0:256].rearrange("p (h d) -> p h d", d=64), reads=[b_Vs], writes=[b_vA])
            dma("sp", kB[:, 0, 768:1024], FM[BLK_KB * 128:(BLK_KB + 1) * 128, S:T], reads=[b_FM], writes=[b_kB])
            for j in range(2):
                dma("sp", kC[:, j, 1024:1280], FM[(BLK_KC + j) * 128:(BLK_KC + j + 1) * 128, S:T], reads=[b_FM], writes=[b_kC])
            for j in range(2):
                dma("sp", vB[:, 6 + j, :, 0:64], Vs[S + j * 128:S + (j + 1) * 128, 256:384].rearrange("p (h d) -> p h d", d=64), reads=[b_Vs], writes=[b_vB])
                dma("sp", vC[:, 8 + j, :, 0:64], Vs[S + j * 128:S + (j + 1) * 128, 384:640].rearrange("p (h d) -> p h d", d=64), reads=[b_Vs], writes=[b_vC])

            def attn_core(q_ap, dk, pbase, nq, klist, scale, comp):
                nk = len(klist)
                items = [(kt, bk, va, bv, ma, bm, q_ap, comp, i == 0, i == nk - 1) for i, (kt, bk, va, bv, ma, bm) in enumerate(klist)]
                attn_chunks([items[c0:c0 + 2] for c0 in range(0, nk, 2)], nq, scale)

            def attn_chunks(chunks, nq, scale):
                pts = {}

                def emit_s(ci):
                    ch = chunks[ci]
                    u = ci % 2
                    for k, it in enumerate(ch):
                        op("pe", lambda e: e.matmul(PS[u][:, k, :nq], lhsT=it[0], rhs=it[6], start=True, stop=True), reads=[it[1], b_qT], writes=[PB[u]], inc=(k == len(ch) - 1))
                    pt, bpt = pT_r.next()
                    pts[ci] = (pt, bpt)
                    op("act", lambda e: e.activation(out=pt[:, 0:len(ch), :nq], in_=PS[u][:, 0:len(ch), :nq], func=AF.Exp, scale=scale), reads=[PB[u]], writes=[bpt])
                    for k, it in enumerate(ch):
                        if it[4] is not None:
                            op("dve", lambda e: e.tensor_tensor(out=pt[:, k, :nq], in0=pt[:, k, :nq], in1=it[4], op=ALU.mult), reads=[bpt, it[5]], writes=[bpt])

                def emit_pv(ci):
                    ch = chunks[ci]
                    pt, bpt = pts.pop(ci)
                    lastchunk = ci == len(chunks) - 1
                    for k, it in enumerate(ch):
                        op("pe", lambda e: e.matmul(PS[2][0:65, it[7], :nq], lhsT=it[2], rhs=pt[:, k, :nq], start=it[8], stop=it[9]),
                           reads=[it[3], bpt], writes=[PB[2]], inc=(lastchunk and k == len(ch) - 1))

                emit_s(0)
                for ci in range(len(chunks)):
                    if ci + 1 < len(chunks):
                        emit_s(ci + 1)
                    emit_pv(ci)

            def to_token_major(ncomp, nq):
                op("act", lambda e: e.activation(out=accs[0:65, 0:ncomp, :nq], in_=PS[2][0:65, 0:ncomp, :nq], func=AF.Copy), reads=[PB[2]], writes=[b_accs])
                nqs = nq // 128
                for cp in range(ncomp):
                    for qs in range(nqs):
                        op("pe", lambda e: e.transpose(out=PS[3][:, cp, qs * 65:(qs + 1) * 65], in_=accs[0:65, cp, qs * 128:(qs + 1) * 128], identity=ident[0:65, 0:65]),
                           reads=[b_accs, b_id], writes=[PB[3]], inc=(cp == ncomp - 1 and qs == nqs - 1))
                return nqs

            for (t0, n, ty) in groups:
                if ty == 1 and not need_ctx:
                    continue
                j0 = t0 // 128
                nqs = n // 128
                for h in range(4):
                    dma("sp", qT[0:64, h, :n], FM[(BLK_QA + h) * 128:(BLK_QA + h) * 128 + 64, t0:t0 + n], reads=[b_FM], writes=[b_qT])
                for j in range(2):
                    dma("sp", qT[:, 4 + j, :n], FM[(BLK_QB + j) * 128:(BLK_QB + j + 1) * 128, t0:t0 + n], reads=[b_FM], writes=[b_qT])
                    dma("sp", qT[:, 6 + j, :n], FM[(BLK_QC + j) * 128:(BLK_QC + j + 1) * 128, t0:t0 + n], reads=[b_FM], writes=[b_qT])
                ctxk = [6, 7]
                if ty == 0:
                    lo, hi = max(j0 - 1, 0), min(j0 + 5, S // 128)
                    dma("sp", kB[:, 0, (lo - j0 + 1) * 128:(hi - j0 + 1) * 128], FM[BLK_KB * 128:(BLK_KB + 1) * 128, lo * 128:hi * 128], reads=[b_FM], writes=[b_kB])
                    for jt in range(lo, hi):
                        dma("sp", vB[:, jt - j0 + 1, :, 0:64], Vs[jt * 128:(jt + 1) * 128, 256:384].rearrange("p (h d) -> p h d", d=64), reads=[b_Vs], writes=[b_vB])
                    lo2, hi2 = max(j0 - 2, 0), min(j0 + 6, S // 128)
                    for j in range(2):
                        dma("sp", kC[:, j, (lo2 - j0 + 2) * 128:(hi2 - j0 + 2) * 128], FM[(BLK_KC + j) * 128:(BLK_KC + j + 1) * 128, lo2 * 128:hi2 * 128], reads=[b_FM], writes=[b_kC])
                    for jt in range(lo2, hi2):
                        dma("sp", vC[:, jt - j0 + 2, :, 0:64], Vs[jt * 128:(jt + 1) * 128, 384:640].rearrange("p (h d) -> p h d", d=64), reads=[b_Vs], writes=[b_vC])
                    g = t0 // 512
                    cls = 1 if g == 0 else (2 if g == NGL - 1 else 0)

                ktiles = list(range(NT)) if ty == 0 else list(range(S // 128, NT))
                for h in range(4):
                    nkt = len(ktiles)
                    chs = [[(kA[32 * cp:32 * cp + 32, h, jt * 128:(jt + 1) * 128], b_kA, vA[:, jt, h, :], b_vA, None, None,
                             qT[32 * cp:32 * cp + 32, h, :n], cp, i == 0, i == nkt - 1) for cp in range(2)] for i, jt in enumerate(ktiles)]
                    attn_chunks(chs, n, 32 ** -0.5)
                    to_token_major(2, n)
                    P3 = PS[3][:, :, 0:nqs * 65].rearrange("p c (q d) -> p c q d", d=65)
                    rz, brz = sm_r.next()
                    op("dve", lambda e: e.reciprocal(out=rz[:, 0:2 * nqs].rearrange("p (c q) -> p c q", c=2), in_=P3[:, :, :, 64]), reads=[PB[3]], writes=[brz])
                    o0, bo0 = o_r.next(); o1, bo1 = o_r.next()
                    op("dve", lambda e: e.tensor_tensor(out=o0[:, :nqs, :], in0=P3[:, 0, :, 0:64], in1=rz[:, 0:nqs].unsqueeze(2).to_broadcast([128, nqs, 64]), op=ALU.mult), reads=[PB[3], brz], writes=[bo0])
                    op("dve", lambda e: e.tensor_tensor(out=o1[:, :nqs, :], in0=P3[:, 1, :, 0:64], in1=rz[:, nqs:2 * nqs].unsqueeze(2).to_broadcast([128, nqs, 64]), op=ALU.mult), reads=[PB[3], brz], writes=[bo1])
                    op("dve", lambda e: e.scalar_tensor_tensor(out=o0[:, :nqs, :], in0=o1[:, :nqs, :], scalar=neglam[:, 0:1], in1=o0[:, :nqs, :], op0=ALU.mult, op1=ALU.add), reads=[bo0, bo1, b_neglam], writes=[bo0])
                    op("dve", lambda e: e.tensor_tensor(out=o1[:, :nqs, :], in0=o0[:, :nqs, :], in1=o0[:, :nqs, :], op=ALU.mult), reads=[bo0], writes=[bo1])
                    s2, bs2 = sm_r.next()
                    op("dve", lambda e: e.tensor_reduce(out=s2[:, 0:nqs], in_=o1[:, :nqs, :], axis=AX.X, op=ALU.add), reads=[bo1], writes=[bs2])
                    op("act", lambda e: e.activation(out=s2[:, 0:nqs], in_=s2[:, 0:nqs], func=AF.Sqrt, bias=epsT[:], scale=1.0 / 64), reads=[bs2, b_eps], writes=[bs2])
                    op("dve", lambda e: e.reciprocal(out=s2[:, 0:nqs], in_=s2[:, 0:nqs]), reads=[bs2], writes=[bs2])
                    op("dve", lambda e: e.tensor_tensor(out=o0[:, :nqs, :], in0=o0[:, :nqs, :], in1=s2[:, 0:nqs].unsqueeze(2).to_broadcast([128, nqs, 64]), op=ALU.mult), reads=[bo0, bs2], writes=[bo0])
                    op("dve", lambda e: e.tensor_tensor(out=Otok[:, :nqs, h * 64:(h + 1) * 64], in0=o0[:, :nqs, :], in1=subw[:].unsqueeze(1).to_broadcast([128, nqs, 64]), op=ALU.mult), reads=[bo0, b_subw], writes=[b_Otok])

                for mix in ("B", "C"):
                    for h in range(4):
                        if mix == "B":
                            k_, g_ = h // 2, h % 2
                            qa = qT[64 * k_:64 * k_ + 64, 4 + g_, :n]
                            kl = []
                            if ty == 0:
                                for rel in range(6):
                                    jt = j0 - 1 + rel
                                    if 0 <= jt < S // 128:
                                        kl.append((kB[64 * k_:64 * k_ + 64, 0, rel * 128:(rel + 1) * 128], b_kB, vB[:, rel, k_, :], b_vB, maskB[:, rel, :], b_maskB))
                            for j in range(2):
                                kl.append((kB[64 * k_:64 * k_ + 64, 0, (6 + j) * 128:(7 + j) * 128], b_kB, vB[:, 6 + j, k_, :], b_vB, None, None))
                            col = 256 + h * 64
                        else:
                            qa = qT[64 * (h % 2):64 * (h % 2) + 64, 6 + h // 2, :n]
                            kl = []
                            if ty == 0:
                                dma("sp", Eg[:], Ed[cls, h], reads=[b_Ed], writes=[b_Eg])
                                for rel in range(8):
                                    jt = j0 - 2 + rel
                                    if 0 <= jt < S // 128:
                                        kl.append((kC[64 * (h % 2):64 * (h % 2) + 64, h // 2, rel * 128:(rel + 1) * 128], b_kC, vC[:, rel, h, :], b_vC, Eg[:, rel, :], b_Eg))
                            for j in range(2):
                                kl.append((kC[64 * (h % 2):64 * (h % 2) + 64, h // 2, (8 + j) * 128:(9 + j) * 128], b_kC, vC[:, 8 + j, h, :], b_vC, None, None))
                            col = 512 + h * 64
                        attn_core(qa, 64, 0, n, kl, 0.125, 0)
                        to_token_major(1, n)
                        P3 = PS[3][:, 0, 0:nqs * 65].rearrange("p (q d) -> p q d", d=65)
                        rz, brz = sm_r.next()
                        if mix == "B":
                            op("dve", lambda e: e.tensor_scalar(out=rz[:, 0:nqs], in0=P3[:, :, 64], scalar1=esink[:, h:h + 1], scalar2=None, op0=ALU.add), reads=[PB[3], b_esink], writes=[brz])
                            op("dve", lambda e: e.reciprocal(out=rz[:, 0:nqs], in_=rz[:, 0:nqs]), reads=[brz], writes=[brz])
                        else:
                            op("dve", lambda e: e.reciprocal(out=rz[:, 0:nqs], in_=P3[:, :, 64]), reads=[PB[3]], writes=[brz])
                        op("dve", lambda e: e.tensor_tensor(out=Otok[:, :nqs, col:col + 64], in0=P3[:, :, 0:64], in1=rz[:, 0:nqs].unsqueeze(2).to_broadcast([128, nqs, 64]), op=ALU.mult), reads=[PB[3], brz], writes=[b_Otok])

                for j in range(2):
                    dma("sp", bgw[:, j, :n], FM[(BLK_BG + j) * 128:(BLK_BG + j + 1) * 128, t0:t0 + n], reads=[b_FM], writes=[b_bgw])
                op("pool", lambda e: e.memset(mw[:], 0.0), writes=[b_mw])
                seq0, seq1 = (0, S) if ty == 0 else (S, T)
                a0, a1 = max(t0 - 1, seq0), min(t0 + n + 1, seq1)
                for j in range(2):
                    dma("sp", mw[:, j, a0 - (t0 - 1):a1 - (t0 - 1)], FM[(BLK_M + j) * 128:(BLK_M + j + 1) * 128, a0:a1], reads=[b_FM], writes=[b_mw])
                for j in range(2):
                    f, bf = f2_r.next()
                    op("dve", lambda e: e.tensor_scalar(out=f[:, :n], in0=mw[:, j, 1:n + 1], scalar1=scw[:, j, 1:2], scalar2=None, op0=ALU.mult), reads=[b_mw, b_scw], writes=[bf])
                    op("dve", lambda e: e.scalar_tensor_tensor(out=f[:, :n], in0=mw[:, j, 0:n], scalar=scw[:, j, 0:1], in1=f[:, :n], op0=ALU.mult, op1=ALU.add), reads=[b_mw, b_scw, bf], writes=[bf])
                    op("dve", lambda e: e.scalar_tensor_tensor(out=f[:, :n], in0=mw[:, j, 2:n + 2], scalar=scw[:, j, 2:3], in1=f[:, :n], op0=ALU.mult, op1=ALU.add), reads=[b_mw, b_scw, bf], writes=[bf])
                    op("dve", lambda e: e.tensor_tensor(out=OT[:, 6 + j, :n], in0=f[:, :n], in1=bgw[:, j, :n], op=ALU.mult), reads=[bf, b_bgw], writes=[b_OT])

                for qs in range(nqs):
                    pt = PS[3][:].bitcast(BF16)
                    for c in range(6):
                        op("pe", lambda e: e.transpose(out=pt[:, 0, c * 128:(c + 1) * 128], in_=Otok[:, qs, c * 128:(c + 1) * 128], identity=identb[:]),
                           reads=[b_Otok, b_idb], writes=[PB[3]], inc=(c == 5))
                    op("act", lambda e: e.activation(out=OT[:, 0:6, qs * 128:(qs + 1) * 128], in_=pt[:, 0, 0:768].rearrange("p (c t) -> p c t", t=128), func=AF.Copy), reads=[PB[3]], writes=[b_OT])
                for qs in range(nqs):
                    for half in range(2):
                        for c in range(8):
                            op("pe", lambda e: e.matmul(PS[2][:, half, :], lhsT=OT[:, c, qs * 128:(qs + 1) * 128], rhs=wout[:, c, half * 512:(half + 1) * 512], start=(c == 0), stop=(c == 7)),
                               reads=[b_OT, b_wout], writes=[PB[2]], inc=(c == 7 and half == 1))
                    xt, bxt = xring.next()
                    dma("sp", xt[:], xres[t0 + qs * 128:t0 + (qs + 1) * 128, :], reads=[b_xres], writes=[bxt])
                    gt, bgt = Gb[0][ty]
                    op("dve", lambda e: e.tensor_tensor(out=sq[:], in0=PS[2][:].rearrange("p a b -> p (a b)"), in1=gt[:], op=ALU.mult), reads=[PB[2], bgt], writes=[b_sq])
                    op("pool", lambda e: e.tensor_tensor(out=xt[:], in0=sq[:], in1=xt[:], op=ALU.add), reads=[b_sq, bxt], writes=[bxt])
                    dma("sp", xmid[t0 + qs * 128:t0 + (qs + 1) * 128, :], xt[:], reads=[bxt], writes=[b_xmid])

            phase()
            xring = Ring(ar, [128, D], F32, 2)
            xnring = Ring(ar, [128, D], BF16, 2)
            sq, b_sq = ar([128, D])
            xh_r = Ring(ar, [128, 8, 258], BF16, 2)
            f1_r = Ring(ar, [128, 2, 258], F32, 2)
            f2_r = Ring(ar, [128, 256], F32, 4)
            hT, b_hT = ar([128, 22, 256], BF16)
            Gb2 = [ar([128, D]) for _ in range(2)]
            fnwT, b_fnw = ar([128, D])
            make_gates(5, Gb2)
            Gb = [None, Gb2]
            if l == DEPTH - 1:
                dma("sp", fnwT[:], fnw[:, :], writes=[b_fnw])
            dma("pool", wbig[:], wupd[l].rearrange("(c p) n -> p c n", p=128), writes=[b_wbig])
            dma("pool", wdn[:], wdnd[l].rearrange("(c p) n -> p c n", p=128), writes=[b_wdn])
            fgroups = [(g * 256, 256, 0, g == 0, g == S // 256 - 1) for g in range(S // 256)]
            if need_ctx:
                fgroups.append((S, 256, 1, True, True))
            prepped = {}

            def prep(gi):
                t0, n, ty, first, last = fgroups[gi]
                xh, bxh = xh_r.next()
                rmsnorm_T(xmid, b_xmid, t0, 2, (A2, modT[:, 24:32, :], ty), b_A2, xh, bxh, 1)
                prepped[gi] = (xh, bxh)

            prep(0)
            for gi, (t0, n, ty, first, last) in enumerate(fgroups):
                xh, bxh = prepped[gi]
                if first:
                    op("pool", lambda e: e.memset(xh[:, :, 0:1], 0.0), writes=[bxh])
                else:
                    xp, bxp = prepped[gi - 1]
                    op("pool", lambda e: e.tensor_copy(out=xh[:, :, 0:1], in_=xp[:, :, 256:257]), reads=[bxp], writes=[bxh])
                if last:
                    op("pool", lambda e: e.memset(xh[:, :, 257:258], 0.0), writes=[bxh])
                if not last:
                    pass
                prepped_next = None
                if not last:
                    prep(gi + 1)
                    xn_, bxn_ = prepped[gi + 1]
                    op("pool", lambda e: e.tensor_copy(out=xh[:, :, 257:258], in_=xn_[:, :, 1:2]), reads=[bxn_], writes=[bxh])
                elif gi + 1 < len(fgroups):
                    prep(gi + 1)
                for j in range(22):
                    u = j % 2
                    for k, colb in enumerate((j * 128, DFF + j * 128)):
                        for c in range(8):
                            op("pe", lambda e: e.matmul(PS[u][:, k, 0:258], lhsT=wbig[:, c, colb:colb + 128], rhs=xh[:, c, 0:258], start=(c == 0), stop=(c == 7)),
                               reads=[b_wbig, bxh], writes=[PB[u]], inc=(c == 7 and k == 1))
                    ub, bub = f1_r.next()
                    op("act", lambda e: e.activation(out=ub[:, :, 0:258], in_=PS[u][:, :, 0:258], func=AF.Copy), reads=[PB[u]], writes=[bub])
                    tg, btg = f2_r.next(); tv, btv = f2_r.next()
                    for k, (tt, btt) in enumerate(((tg, btg), (tv, btv))):
                        jj = j + 22 * k
                        op("act", lambda e: e.activation(out=tt[:, 0:256], in_=ub[:, k, 1:257], func=AF.Identity, scale=fcw[:, jj, 1:2], bias=fcb[:, jj:jj + 1]), reads=[bub, b_fcw, b_fcb], writes=[btt])
                        op("dve", lambda e: e.scalar_tensor_tensor(out=tt[:, 0:256], in0=ub[:, k, 0:256], scalar=fcw[:, jj, 0:1], in1=tt[:, 0:256], op0=ALU.mult, op1=ALU.add), reads=[bub, b_fcw, btt], writes=[btt])
                        op("dve", lambda e: e.scalar_tensor_tensor(out=tt[:, 0:256], in0=ub[:, k, 2:258], scalar=fcw[:, jj, 2:3], in1=tt[:, 0:256], op0=ALU.mult, op1=ALU.add), reads=[bub, b_fcw, btt], writes=[btt])
                    op("act", lambda e: e.activation(out=tg[:, 0:256], in_=tg[:, 0:256], func=AF.Silu), reads=[btg], writes=[btg])
                    op("pool", lambda e: e.tensor_tensor(out=hT[:, j, :], in0=tg[:, 0:256], in1=tv[:, 0:256], op=ALU.mult), reads=[btg, btv], writes=[b_hT])
                for i in range(2):
                    for half in range(2):
                        for j in range(22):
                            op("pe", lambda e: e.matmul(PS[2][:, half, :], lhsT=hT[:, j, i * 128:(i + 1) * 128], rhs=wdn[:, j, half * 512:(half + 1) * 512], start=(j == 0), stop=(j == 21)),
                               reads=[b_hT, b_wdn], writes=[PB[2]], inc=(j == 21 and half == 1))
                    xt, bxt = xring.next()
                    dma("sp", xt[:], xmid[t0 + i * 128:t0 + (i + 1) * 128, :], reads=[b_xmid], writes=[bxt])
                    gt, bgt = Gb[1][ty]
                    op("dve", lambda e: e.tensor_tensor(out=sq[:], in0=PS[2][:].rearrange("p a b -> p (a b)"), in1=gt[:], op=ALU.mult), reads=[PB[2], bgt], writes=[b_sq])
                    op("pool", lambda e: e.tensor_tensor(out=xt[:], in0=sq[:], in1=xt[:], op=ALU.add), reads=[b_sq, bxt], writes=[bxt])
                    if l < DEPTH - 1:
                        dma("sp", xres[t0 + i * 128:t0 + (i + 1) * 128, :], xt[:], reads=[bxt], writes=[b_xres])
                    else:
                        ss, bss = ss_r.next(); rs, brs = rstd_r.next()
                        op("act", lambda e: e.activation(out=sq[:], in_=xt[:], func=AF.Square, accum_out=ss[:]), reads=[bxt], writes=[b_sq, bss])
                        op("act", lambda e: e.activation(out=rs[:], in_=ss[:], func=AF.Sqrt, bias=epsT[:], scale=1.0 / D), reads=[bss, b_eps], writes=[brs])
                        op("dve", lambda e: e.reciprocal(out=rs[:], in_=rs[:]), reads=[brs], writes=[brs])
                        op("dve", lambda e: e.scalar_tensor_tensor(out=xt[:], in0=xt[:], scalar=rs[:, 0:1], in1=fnwT[:], op0=ALU.mult, op1=ALU.mult), reads=[bxt, brs, b_fnw], writes=[bxt])
                        dma("sp", out_d[t0 + i * 128:t0 + (i + 1) * 128, :], xt[:], reads=[bxt], writes=[b_out])
        S_.finish([b_out, b_xres, b_xmid], "sp")
        build.n_instr = S_.n
    return nc


def host_inputs(S, DEPTH, b, x, c, ctx, c_ctx, norm_mix_w, norm_ffn_w, w_mod, b_mod, w_in, w_out,
                diff_lambda_q1, diff_lambda_k1, diff_lambda_q2, diff_lambda_k2, diff_subln_w,
                swa_sink, na_rpb, sconv_w, ffn_w_up, ffn_conv_w, ffn_conv_b, ffn_w_down, final_norm_w, shared):
    f = np.float32
    plan = fm_plan()
    if "w1" not in shared:
        cols = []
        for rows, m, p, kind in plan:
            cols.append(m)
            if p is not None:
                cols.append(p)
        cols.append(VCOLS)
        cols = np.concatenate(cols)
        shared["w1"] = np.ascontiguousarray(np.asarray(w_in, f)[:, :, cols])
        shared["nmw"] = np.ascontiguousarray(np.asarray(norm_mix_w, f).reshape(DEPTH, 8, 128).transpose(0, 2, 1))
        shared["nfw"] = np.ascontiguousarray(np.asarray(norm_ffn_w, f).reshape(DEPTH, 8, 128).transpose(0, 2, 1))
        shared["fnw"] = np.ascontiguousarray(np.broadcast_to(np.asarray(final_norm_w, f)[None, :], (128, D)))
        shared["bmod"] = np.ascontiguousarray(np.asarray(b_mod, f).reshape(DEPTH, 48, 128).transpose(0, 2, 1))
        lam = np.stack([diff_lambda_q1, diff_lambda_k1, diff_lambda_q2, diff_lambda_k2], axis=1).astype(f)
        shared["lam"] = np.ascontiguousarray(np.broadcast_to(lam[:, None], (DEPTH, 128, 4, 32)))
        shared["subln"] = np.ascontiguousarray(np.broadcast_to(np.asarray(diff_subln_w, f)[:, None], (DEPTH, 128, 64)))
        shared["sink"] = np.ascontiguousarray(np.broadcast_to(np.asarray(swa_sink, f)[:, None], (DEPTH, 128, 4)))
        rp = np.zeros((DEPTH, 4, 15, 128), f)
        rp[..., 48:79] = np.asarray(na_rpb, f)
        shared["rpb"] = rp
        shared["sconv"] = np.ascontiguousarray(np.asarray(sconv_w, f).reshape(DEPTH, 3, 2, 128).transpose(0, 3, 2, 1))
        shared["fcw"] = np.ascontiguousarray(np.asarray(ffn_conv_w, f).reshape(DEPTH, 3, 44, 128).transpose(0, 3, 2, 1))
        shared["fcb"] = np.ascontiguousarray(np.asarray(ffn_conv_b, f).reshape(DEPTH, 44, 128).transpose(0, 2, 1))
        shared["ident"] = np.eye(128, dtype=f)
        for i, t in enumerate(rope_tables(S)):
            shared[f"rope{i}"] = np.ascontiguousarray(t)
        k = np.arange(128)[:, None]
        q = np.arange(512)[None, :]
        mb = np.stack([(np.abs((rel - 1) * 128 + k - q) <= 128) for rel in range(6)]).astype(f)
        shared["maskB"] = mb.astype(ml_dtypes.bfloat16)
        cq = np.arange(64)
        cs = np.clip(cq - 8, 0, 48)
        ck = np.arange(64)
        cm = ((ck[:, None] >= cs[None, :]) & (ck[:, None] < cs[None, :] + 16)).astype(f)
        shared["colmask"] = np.ascontiguousarray(np.concatenate([cm, cm], axis=0))
        shared["wmod"] = np.ascontiguousarray(np.asarray(w_mod, f))
        shared["wout"] = np.ascontiguousarray(np.asarray(w_out, f))
        shared["wup"] = np.ascontiguousarray(np.asarray(ffn_w_up, f))
        shared["wdn"] = np.ascontiguousarray(np.asarray(ffn_w_down, f))
    d = dict(shared)
    d["x"] = np.ascontiguousarray(np.asarray(x[b], f))
    d["ctx"] = np.ascontiguousarray(np.asarray(ctx[b], f))
    cv = np.stack([np.asarray(c[b], f), np.asarray(c_ctx, f)], axis=-1)
    d["cvec"] = np.ascontiguousarray(cv.reshape(8, 128, 2).transpose(1, 0, 2))
    return d


def kernel(**inputs):
    x = np.asarray(inputs["x"])
    B, S, _ = x.shape
    DEPTH = np.asarray(inputs["w_in"]).shape[0]
    nc = build(S, DEPTH)
    shared = {}
    in_maps = [host_inputs(S, DEPTH, b, shared=shared, **inputs) for b in range(B)]
    res = run_bass_kernel_spmd(nc, in_maps, core_ids=list(range(B)))
    return np.stack([np.asarray(r["out"], np.float32) for r in res.results], axis=0)
```
